# Optimizing a Trainium2 kernel written in Bass

```python
import jax, jax.numpy as jnp
from jax import lax
import numpy as np

D_MODEL = 1024
BATCH = 2
SEQ = 8192
DEPTH = 2

CHUNK = 64
EPS = 1e-6
F_MIN = 1e-6
DKA = 128
HA = D_MODEL // DKA
WA = HA * DKA
HB = 8
DKB = D_MODEL // HB
DVB = 2 * DKB
WB_QK = HB * DKB
WB_V = HB * DVB
ROPE_BASE = 10000.0
RET_GN_EPS = 1e-5
NC = 64
HC = D_MODEL // NC
WC = HC * NC
W_LORA = 64
A_LORA = 64
RWKV_GN_EPS = 64e-5
C_SHIFT = 3 * WC + 2 * W_LORA + 2 * A_LORA

IN_SIZES = (WA, WA, WA, WA, WA,
            WB_QK, WB_QK, WB_V, WB_V,
            C_SHIFT, WC,
            3 * D_MODEL)
IN_OFFSETS = tuple(int(v) for v in np.cumsum(IN_SIZES)[:-1])
N_IN = int(sum(IN_SIZES))

kernel_name = "hybrid_hgrn2_retention_rwkv7_bidir"


def rmsnorm(x, g, eps=EPS):
    xf = x.astype(jnp.float32)
    y = xf * lax.rsqrt(jnp.mean(xf * xf, axis=-1, keepdims=True) + eps)
    return y * g


def head_groupnorm(o, g, b, eps):
    of = o.astype(jnp.float32)
    mu = jnp.mean(of, axis=-1, keepdims=True)
    var = jnp.mean(jnp.square(of - mu), axis=-1, keepdims=True)
    return (of - mu) * lax.rsqrt(var + eps) * g + b


def heads(z, h):
    b, t, w = z.shape
    return z.reshape(b, t, h, w // h).transpose(0, 2, 1, 3)


def chunk_gla(q, k, v, log_f, strict):
    b, h, t, dk = q.shape
    dv = v.shape[-1]
    n = t // CHUNK

    def chunks(a):
        return a.reshape(b, h, n, CHUNK, a.shape[-1]).transpose(2, 0, 1, 3, 4)

    qc, kc, vc = chunks(q), chunks(k), chunks(v)
    bc = jnp.cumsum(chunks(log_f.astype(jnp.float32)), axis=3)
    mask = jnp.tril(jnp.ones((CHUNK, CHUNK), dtype=bool), -1 if strict else 0)[:, :, None]

    def step(state, inp):
        qi, ki, vi, bi = inp
        diff = bi[:, :, :, None, :] - bi[:, :, None, :, :]
        decay = jnp.where(mask, jnp.exp(jnp.where(mask, diff, 0.0)), 0.0)
        scores = jnp.sum(qi[:, :, :, None, :] * ki[:, :, None, :, :] * decay, axis=-1)
        o = (jnp.einsum("bhij,bhjv->bhiv", scores, vi)
             + jnp.einsum("bhik,bhkv->bhiv", qi * jnp.exp(bi), state))
        b_last = bi[:, :, -1:, :]
        state = (jnp.exp(b_last[:, :, 0, :])[..., None] * state
                 + jnp.einsum("bhjk,bhjv->bhkv", ki * jnp.exp(b_last - bi), vi))
        return state, o

    s0 = jnp.zeros((b, h, dk, dv), jnp.float32)
    _, o = lax.scan(step, s0, (qc, kc, vc, bc))
    return o.transpose(1, 2, 0, 3, 4).reshape(b, h, t, dv)


def hgrn2_mixer(q, f_fwd, f_bwd, i, lb, norm_g):
    b, t, _ = q.shape
    qh = heads(jax.nn.silu(q), HA)
    ih = heads(i, HA)
    o = 0.0
    for d, f_raw in enumerate((f_fwd, f_bwd)):
        f_raw = f_raw.astype(jnp.float32)
        f = lb[d] + (1.0 - lb[d]) * jax.nn.sigmoid(f_raw)
        log_f = jnp.log(jnp.maximum(f, F_MIN))
        k = (1.0 - lb[d]) * jax.nn.sigmoid(-f_raw)
        args = (qh, heads(k, HA), ih, heads(log_f, HA))
        if d == 0:
            o = o + chunk_gla(*args, strict=False)
        else:
            o = o + jnp.flip(chunk_gla(*[jnp.flip(z, 2) for z in args], strict=False), 2)
    o = rmsnorm(o.transpose(0, 2, 1, 3), norm_g)
    return o.reshape(b, t, WA).astype(q.dtype)


def rotary(z):
    t, dh = z.shape[2], z.shape[3]
    inv = ROPE_BASE ** (-jnp.arange(0, dh, 2, dtype=jnp.float32) / dh)
    ang = jnp.arange(t, dtype=jnp.float32)[:, None] * inv
    cos, sin = jnp.cos(ang), jnp.sin(ang)
    z1, z2 = z[..., : dh // 2], z[..., dh // 2:]
    return jnp.concatenate([z1 * cos - z2 * sin, z2 * cos + z1 * sin], axis=-1).astype(z.dtype)


def retention_mixer(q, k, v, gn_g, gn_b):
    b, t, _ = q.shape
    qh = rotary(heads(q, HB))
    kh = rotary(heads(k, HB)) * (DKB ** -0.5)
    vh = heads(v, HB)
    log_gamma = jnp.log1p(-jnp.exp2(-5.0 - jnp.arange(HB, dtype=jnp.float32)))
    log_f = jnp.broadcast_to(log_gamma[None, :, None, None], (b, HB, t, 1))
    fwd = chunk_gla(qh, kh, vh, log_f, strict=False)
    bwd = jnp.flip(chunk_gla(jnp.flip(qh, 2), jnp.flip(kh, 2), jnp.flip(vh, 2), log_f, strict=True), 2)
    o = head_groupnorm((fwd + bwd).transpose(0, 2, 1, 3), gn_g, gn_b, RET_GN_EPS)
    return o.reshape(b, t, WB_V).astype(q.dtype)


def centred_shift(p, mu):
    prev = jnp.pad(p[:, :-1], ((0, 0), (1, 0), (0, 0)))
    nxt = jnp.pad(p[:, 1:], ((0, 0), (0, 1), (0, 0)))
    return p + mu[0] * (prev - p) + mu[1] * (nxt - p)


def rwkv7_scan(r, decay, k, v, kk, a, reverse):
    def tm(z):
        return jnp.moveaxis(z.astype(jnp.float32), 1, 0)

    def step(S, inp):
        r_t, w_t, k_t, v_t, kk_t, a_t = inp
        sa = jnp.einsum("bhvk,bhk->bhv", S, -kk_t)
        S = (S * w_t[:, :, None, :]
             + sa[..., None] * (kk_t * a_t)[:, :, None, :]
             + v_t[..., None] * k_t[:, :, None, :])
        return S, jnp.einsum("bhvk,bhk->bhv", S, r_t)

    s0 = jnp.zeros((r.shape[0], HC, NC, NC), jnp.float32)
    _, o = lax.scan(step, s0, tuple(tm(z) for z in (r, decay, k, v, kk, a)), reverse=reverse)
    return jnp.moveaxis(o, 0, 1)


def rwkv7_mixer(c_s, mu, w0, w_lora_b, a0, a_lora_b, k_k, k_a, r_k, ln_g, ln_b):
    b, t, _ = c_s.shape
    s = centred_shift(c_s, mu)
    r, k, v, codes = jnp.split(s, (WC, 2 * WC, 3 * WC), axis=-1)
    w_code = codes[..., : 2 * W_LORA].reshape(b, t, 2, W_LORA)
    a_code = codes[..., 2 * W_LORA:].reshape(b, t, 2, A_LORA)

    def hd(z):
        return z.reshape(b, t, HC, NC)

    kk = hd(k * k_k).astype(jnp.float32)
    kk = kk * lax.rsqrt(jnp.sum(kk * kk, axis=-1, keepdims=True) + 1e-12)
    o = 0.0
    bonus = 0.0
    for d in range(2):
        w = (w0[d] + jnp.tanh(w_code[:, :, d]) @ w_lora_b[d]).astype(jnp.float32)
        decay = jnp.exp(-jnp.exp(-jax.nn.softplus(-w) - 0.5))
        a = jax.nn.sigmoid((a0[d] + a_code[:, :, d] @ a_lora_b[d]).astype(jnp.float32))
        k_d = k * (1.0 + (a - 1.0) * k_a)
        o = o + rwkv7_scan(hd(r), hd(decay), hd(k_d), hd(v), kk, hd(a), reverse=(d == 1))
        bonus = bonus + jnp.sum(hd(r * k_d * r_k), axis=-1, keepdims=True) * hd(v)
    y = head_groupnorm(o, ln_g, ln_b, RWKV_GN_EPS) + bonus
    return y.reshape(b, t, WC).astype(c_s.dtype)


def setup_inputs(seed: int = 0) -> dict:
    key = jax.random.key(seed)
    ks = jax.random.split(key, 24)
    f32 = jnp.float32

    def nrm(k, shape, scale):
        return scale * jax.random.normal(k, shape, f32)

    def gain(k, shape):
        return 1.0 + nrm(k, shape, 0.02)

    return {
        "x": nrm(ks[0], (BATCH, SEQ, D_MODEL), 1.0),
        "norm_g": gain(ks[1], (DEPTH, D_MODEL)),
        "w_in": nrm(ks[2], (DEPTH, D_MODEL, N_IN), D_MODEL ** -0.5),
        "hgrn_lb_logits": nrm(ks[3], (DEPTH, 2, WA), 0.1),
        "hgrn_norm_g": gain(ks[4], (DEPTH, HA, DKA)),
        "ret_norm_g": gain(ks[5], (DEPTH, HB, DVB)),
        "ret_norm_b": nrm(ks[6], (DEPTH, HB, DVB), 0.02),
        "rwkv_mu": jax.random.uniform(ks[7], (DEPTH, 2, C_SHIFT), f32, 0.0, 0.5),
        "rwkv_w0": jax.random.uniform(ks[8], (DEPTH, 2, WC), f32, -6.0, 1.0),
        "rwkv_w_lora_b": nrm(ks[9], (DEPTH, 2, W_LORA, WC), 0.5 * W_LORA ** -0.5),
        "rwkv_a0": nrm(ks[10], (DEPTH, 2, WC), 0.1),
        "rwkv_a_lora_b": nrm(ks[11], (DEPTH, 2, A_LORA, WC), 0.5 * A_LORA ** -0.5),
        "rwkv_k_k": 0.85 + nrm(ks[12], (DEPTH, WC), 0.02),
        "rwkv_k_a": 1.0 + nrm(ks[13], (DEPTH, WC), 0.02),
        "rwkv_r_k": nrm(ks[14], (DEPTH, WC), 0.1),
        "rwkv_ln_g": gain(ks[15], (DEPTH, HC, NC)),
        "rwkv_ln_b": nrm(ks[16], (DEPTH, HC, NC), 0.02),
        "w_branch_a": nrm(ks[17], (DEPTH, WA, D_MODEL), WA ** -0.5),
        "w_branch_b": nrm(ks[18], (DEPTH, WB_V, D_MODEL), WB_V ** -0.5),
        "w_branch_c": nrm(ks[19], (DEPTH, WC, D_MODEL), WC ** -0.5),
        "w_out": nrm(ks[20], (DEPTH, D_MODEL, D_MODEL), D_MODEL ** -0.5),
        "final_norm_g": gain(ks[21], (D_MODEL,)),
    }


def reference(x, norm_g, w_in, hgrn_lb_logits, hgrn_norm_g, ret_norm_g, ret_norm_b,
              rwkv_mu, rwkv_w0, rwkv_w_lora_b, rwkv_a0, rwkv_a_lora_b, rwkv_k_k, rwkv_k_a,
              rwkv_r_k, rwkv_ln_g, rwkv_ln_b, w_branch_a, w_branch_b, w_branch_c, w_out,
              final_norm_g):
    b, t, _ = x.shape
    p_lb = jax.nn.softmax(hgrn_lb_logits.astype(jnp.float32), axis=0)
    lbs = jnp.cumsum(p_lb, axis=0) - p_lb[0:1]
    for l in range(DEPTH):
        h = rmsnorm(x, norm_g[l]).astype(x.dtype)
        proj = h @ w_in[l]
        (a_q, a_ff, a_fb, a_i, a_g, b_q, b_k, b_v, b_g, c_s, c_g, merge) = jnp.split(
            proj, IN_OFFSETS, axis=-1)
        y_a = hgrn2_mixer(a_q, a_ff, a_fb, a_i, lbs[l], hgrn_norm_g[l]) * jax.nn.silu(a_g)
        y_b = retention_mixer(b_q, b_k, b_v, ret_norm_g[l], ret_norm_b[l]) * jax.nn.silu(b_g)
        y_c = rwkv7_mixer(c_s, rwkv_mu[l], rwkv_w0[l], rwkv_w_lora_b[l], rwkv_a0[l],
                          rwkv_a_lora_b[l], rwkv_k_k[l], rwkv_k_a[l], rwkv_r_k[l],
                          rwkv_ln_g[l], rwkv_ln_b[l]) * jax.nn.silu(c_g)
        gates = jax.nn.sigmoid(merge.reshape(b, t, 3, D_MODEL))
        mixed = (gates[:, :, 0] * (y_a @ w_branch_a[l])
                 + gates[:, :, 1] * (y_b @ w_branch_b[l])
                 + gates[:, :, 2] * (y_c @ w_branch_c[l]))
        x = x + (mixed @ w_out[l]).astype(x.dtype)
    return rmsnorm(x, final_norm_g).astype(x.dtype)
```

```python
import numpy as np
import concourse.bass as bass
import concourse.mybir as mybir
from concourse.bass_utils import run_bass_kernel_spmd

F32 = mybir.dt.float32
BF16 = mybir.dt.bfloat16
AF = mybir.ActivationFunctionType
ALU = mybir.AluOpType
AX = mybir.AxisListType

EPOCH = 30000


class TT:
    def __init__(self, fw, h, name):
        self.fw = fw
        self.h = h
        self.name = name
        self.last_w = None
        self.readers = {}
        self.dsem = None
        self.dcount = 0
        self.is_psum = False

    def __getitem__(self, idx):
        return self.h[idx]

    def ap(self):
        return self.h[:]


class Eng:
    def __init__(self, fw, name, handle):
        self.fw = fw
        self.name = name
        self.h = handle
        self.sem = None
        self.count = 0
        self.total = 0
        self.waited = {}

    def new_token(self):
        if self.sem is None or self.count >= EPOCH:
            self.sem = self.fw.new_sem(self.name)
            self.count = 0
        self.count += 1
        self.total += 1
        return (self.sem, self.count)

    def wait(self, tok):
        sem, val = tok[0], tok[1]
        k = id(sem)
        if self.waited.get(k, (None, 0))[1] >= val:
            return
        self.h.wait_ge(sem, val)
        self.waited[k] = (sem, val)


class FW:
    def __init__(self, nc):
        self.nc = nc
        self.nsem = 0
        self.pe = Eng(self, "pe", nc.tensor)
        self.dve = Eng(self, "dve", nc.vector)
        self.act = Eng(self, "act", nc.scalar)
        self.pool = Eng(self, "pool", nc.gpsimd)
        self.sp = Eng(self, "sp", nc.sync)
        self.engs = [self.pe, self.dve, self.act, self.pool, self.sp]
        import os
        self.pv = self.dve if os.environ.get("NOPOOL") == "1" else self.pool
        self.opcount = 0
        self.oplog = os.environ.get("OPLOG") == "1"
        self.stopn = int(os.environ.get("STOPN", "1000000000"))
        self.out_tokens = []
        self.dma_pending = {}
        self.stack = None

    def new_sem(self, name):
        self.nsem += 1
        cm = self.nc.semaphore(f"s{self.nsem}_{name}")
        return cm.__enter__()

    def sb(self, name, shape, dt):
        if getattr(self, "stack", None) is not None:
            h = self.stack.enter_context(self.nc.sbuf_tensor(name, list(shape), dt))
            return TT(self, h, name)
        return TT(self, self.nc.alloc_sbuf_tensor(name, list(shape), dt), name)

    def push_scope(self):
        import contextlib, os
        self.noscope = os.environ.get("NOSCOPE") == "1"
        self.nobarrier = os.environ.get("NOBARRIER") == "1"
        if not self.noscope:
            self.stack = contextlib.ExitStack()

    def pop_scope(self):
        if not self.nobarrier:
            self.barrier()
        if not self.noscope:
            self.stack.close()
            self.stack = None

    def barrier(self):
        toks = []
        for e in self.engs:
            if e.sem is not None and e.count > 0:
                toks.append((e.sem, e.count, e))
        toks.extend(self.dma_pending.values())
        for e in self.engs:
            for t in toks:
                if t[2] is e:
                    continue
                e.wait(t)
        self.dma_pending = {}

    def ps(self, name, shape, dt=F32):
        t = TT(self, self.nc.alloc_psum_tensor(name, list(shape), dt), name)
        t.is_psum = True
        return t

    def dram(self, name, shape, dt, kind="Internal"):
        return TT(self, self.nc.dram_tensor(name, list(shape), dt, kind=kind), name)

    def _deps(self, eng, outs, ins):
        deps = []
        for t in ins:
            if t.last_w is not None:
                deps.append(t.last_w)
            if t.is_psum:
                deps.extend(d for d in t.readers.values() if d[2] is not eng)
        for t in outs:
            if t.last_w is not None:
                deps.append(t.last_w)
            deps.extend(t.readers.values())
        return deps

    def op(self, eng, fn, outs=(), ins=(), skip_self=False):
        self.opcount += 1
        if self.opcount > self.stopn:
            return None
        if self.oplog:
            print("OP", self.opcount, eng.name, [t.name for t in outs], [t.name for t in ins])
        deps = self._deps(eng, outs, ins)
        for d in deps:
            if skip_self and d[2] is eng:
                continue
            eng.wait(d)
        ins_ = fn()
        tok = eng.new_token()
        ins_.then_inc(tok[0], 1)
        tok = (tok[0], tok[1], eng)
        for t in outs:
            t.last_w = tok
            t.readers = {}
        for t in ins:
            if t not in outs:
                t.readers[id(tok[0])] = tok
        return tok

    def mm(self, out_t, out_ap, lhsT_t, lhsT_ap, rhs_t, rhs_ap, start=True, stop=True):
        nc = self.nc
        return self.op(self.pe, lambda: nc.tensor.matmul(out_ap, lhsT_ap, rhs_ap, start=start, stop=stop),
                       outs=[out_t], ins=[lhsT_t, rhs_t], skip_self=True)

    def transpose(self, out_t, out_ap, in_t, in_ap, ident_t, ident_ap):
        nc = self.nc
        return self.op(self.pe, lambda: nc.tensor.transpose(out_ap, in_ap, ident_ap),
                       outs=[out_t], ins=[in_t, ident_t], skip_self=True)

    def dma(self, q, out_ap, in_ap, out_t=None, in_t=None, **kw):
        self.opcount += 1
        if self.opcount > self.stopn:
            return None
        if self.oplog:
            print("DMA", self.opcount, q.name, out_t.name if out_t else None, in_t.name if in_t else None)
        outs = [out_t] if out_t is not None else []
        ins = [in_t] if in_t is not None else []
        deps = self._deps(q, outs, ins)
        for d in deps:
            q.wait(d)
        owner = out_t if out_t is not None else in_t
        if owner.dsem is not None and owner.dcount >= 16 * 1500:
            q.wait((owner.dsem, owner.dcount))
            owner.dsem = None
            owner.dcount = 0
        if owner.dsem is None:
            owner.dsem = self.new_sem("d_" + owner.name)
        ins_ = q.h.dma_start(out=out_ap, in_=in_ap, **kw)
        owner.dcount += 16
        ins_.then_inc(owner.dsem, 16)
        tok = (owner.dsem, owner.dcount, None)
        self.dma_pending[id(owner.dsem)] = tok
        for t in outs:
            t.last_w = tok
            t.readers = {}
        for t in ins:
            t.readers[id(tok[0])] = tok
        return tok

    def finish(self, toks):
        for t in toks:
            if t is not None:
                self.sp.wait(t)
        for e in self.engs:
            if e.sem is not None and e.count > 0 and e is not self.sp:
                self.sp.wait((e.sem, e.count, e))
        for t in self.dma_pending.values():
            self.sp.wait(t)


D = 1024
KC = 8
SCALE_B = float(np.float32(128 ** -0.5))
EPS = float(np.float32(1e-6))


class Consts:
    def __init__(self):
        self.cols = {}
        self.n = 0
        self.arrs = []

    def add(self, name, arr):
        arr = np.asarray(arr, dtype=np.float32)
        assert arr.shape[0] == 128 and arr.ndim == 2, (name, arr.shape)
        self.cols[name] = (self.n, self.n + arr.shape[1])
        self.n += arr.shape[1]
        self.arrs.append(arr)

    def array(self):
        return np.ascontiguousarray(np.concatenate(self.arrs, axis=1))


def bcast(v):
    return np.broadcast_to(np.asarray(v, np.float32).reshape(1, -1), (128, np.asarray(v).size)).copy()


def consts_common():
    c = Consts()
    c.add("ident", np.eye(128))
    return c


def consts_B(c, head, gn_g, gn_b):
    gam = 1.0 - 2.0 ** (-5.0 - head)
    i = np.arange(128)
    c.add("B_Pm", np.eye(128)[(i + 64) % 128].T.copy() if False else np.roll(np.eye(128), 64, axis=0))
    c.add("B_D", gam ** np.abs(i[:, None] - i[None, :]))
    c.add("B_rowf", bcast(gam ** (i + 1.0)))
    c.add("B_rowb", bcast(gam ** (128.0 - i)))
    c.add("B_colf", (gam ** (127.0 - i)).reshape(128, 1))
    c.add("B_colb", (gam ** (i * 1.0)).reshape(128, 1))
    c.add("B_gL", np.full((128, 1), gam ** 128.0))
    c.add("B_gng", bcast(gn_g))
    c.add("B_gnb", bcast(gn_b))
    c.add("B_eps", np.full((128, 1), 1e-5))


def rotary_tables(T):
    inv = 10000.0 ** (-np.arange(0, 128, 2, dtype=np.float32) / 128)
    ang = np.arange(T, dtype=np.float32)[:, None] * inv
    cos, sin = np.cos(ang).T, np.sin(ang).T
    C = np.concatenate([cos, cos], 0).astype(np.float32)
    S = np.concatenate([-sin, sin], 0).astype(np.float32)
    return np.ascontiguousarray(C), np.ascontiguousarray(S)


class LA:
    def __init__(self, T, NB, ncst, phases=("0", "B")):
        self.T, self.NB = T, NB
        self.NT = T // 128
        nc = self.nc = bass.Bass("TRN2", target_bir_lowering=False)
        fw = self.fw = FW(nc)
        NTOK = T * NB
        self.x = fw.dram("x", [NTOK, D], F32, kind="ExternalInput")
        self.g_in = fw.dram("g_in", [128, KC], F32, kind="ExternalInput")
        self.cst_d = fw.dram("cst", [128, ncst], F32, kind="ExternalInput")
        self.wB_d = fw.dram("wB", [128, KC, 768], F32, kind="ExternalInput")
        self.wA_d = fw.dram("wA", [128, KC, 640], F32, kind="ExternalInput")
        self.wC_d = fw.dram("wC", [128, KC, 768], F32, kind="ExternalInput")
        self.rotC = fw.dram("rotC", [128, T], F32, kind="ExternalInput")
        self.rotS = fw.dram("rotS", [128, T], F32, kind="ExternalInput")
        self.y_o = fw.dram("y_o", [NTOK, 512], BF16, kind="ExternalOutput")
        self.hT_d = fw.dram("hT_d", [NB * self.NT, 128, KC * 128], BF16)
        self.cst = fw.sb("cst_sb", [128, ncst], F32)
        for c0 in range(0, ncst, 2048):
            c1 = min(ncst, c0 + 2048)
            fw.dma(fw.sp, self.cst[:, c0:c1], self.cst_d[:, c0:c1], out_t=self.cst, in_t=self.cst_d)
        self.g_sb = fw.sb("g_sb", [128, KC], F32)
        fw.dma(fw.sp, self.g_sb[:], self.g_in[:], out_t=self.g_sb, in_t=self.g_in)
        self.ps = [fw.ps(f"ps{i}", [128, 512], F32) for i in range(6)]
        self.psb = [fw.ps(f"psb{i}", [128, 1024], BF16) for i in range(2)]
        self.ps_i = 0
        self.stage = [fw.sb(f"wstage{i}", [128, KC, 128], F32) for i in range(2)]
        self.stage_i = 0

    def C(self, name, sub=None):
        a, b = self.ccols[name]
        return self.cst[:, a:b]

    def to_bf16(self, name, shape, src_ap, src_t):
        t = self.fw.sb(name, shape, BF16)
        nc = self.nc
        self.fw.op(self.fw.dve, lambda: nc.vector.tensor_copy(t[:], src_ap), outs=[t], ins=[src_t])
        return t

    def load_w(self, name, wd, ncols):
        fw, nc = self.fw, self.nc
        wb = fw.sb(name, [128, KC, ncols], BF16)
        for c0 in range(0, ncols, 128):
            st = self.stage[self.stage_i % 2]
            self.stage_i += 1
            fw.dma(fw.sp, st[:], wd[:, :, c0:c0 + 128], out_t=st, in_t=wd)
            fw.op(fw.act, lambda: nc.scalar.copy(wb[:, :, c0:c0 + 128], st[:]), outs=[wb], ins=[st])
        return wb

    def nps(self):
        p = self.ps[self.ps_i % len(self.ps)]
        self.ps_i += 1
        return p

    def phase0(self):
        fw, nc = self.fw, self.nc
        ident = self.ident
        fw.push_scope()
        xt = [fw.sb(f"xt{i}", [128, D], F32) for i in range(2)]
        sq = fw.sb("sq", [128, D], BF16)
        xn = [fw.sb(f"xn{i}", [128, D], BF16) for i in range(2)]
        hT = [fw.sb(f"hT0_{i}", [128, KC * 128], BF16) for i in range(2)]
        ssq = [fw.sb(f"ssq{i}", [128, 1], F32) for i in range(2)]
        g_sb = self.g_sb
        for i in range(self.NB * self.NT):
            xb, xnb, hb, sb = xt[i % 2], xn[i % 2], hT[i % 2], ssq[i % 2]
            tp = self.psb[i % 2]
            fw.dma(fw.sp, xb[:], self.x[i * 128:(i + 1) * 128, :], out_t=xb, in_t=self.x)
            fw.op(fw.act, lambda: nc.scalar.activation(sq[:], xb[:], AF.Square, accum_out=sb[:]), outs=[sq, sb], ins=[xb])
            fw.op(fw.dve, lambda: nc.vector.tensor_scalar(sb[:], sb[:], 1.0 / D, EPS, ALU.mult, ALU.add), outs=[sb], ins=[sb])
            fw.op(fw.act, lambda: nc.scalar.sqrt(sb[:], sb[:]), outs=[sb], ins=[sb])
            fw.op(fw.dve, lambda: nc.vector.reciprocal(sb[:], sb[:]), outs=[sb], ins=[sb])
            fw.op(fw.dve, lambda: nc.vector.tensor_scalar(xnb[:], xb[:], sb[:, 0:1], None, ALU.mult), outs=[xnb], ins=[xb, sb])
            for kc in range(KC):
                fw.transpose(tp, tp[:, kc * 128:(kc + 1) * 128], xnb, xnb[:, kc * 128:(kc + 1) * 128], ident, ident[:])
            for kc in range(KC):
                e = fw.act if kc % 2 == 0 else fw.dve
                if e is fw.act:
                    fw.op(e, lambda: nc.scalar.activation(hb[:, kc * 128:(kc + 1) * 128], tp[:, kc * 128:(kc + 1) * 128], AF.Copy, scale=g_sb[:, kc:kc + 1]), outs=[hb], ins=[tp, g_sb])
                else:
                    fw.op(e, lambda: nc.vector.tensor_scalar(hb[:, kc * 128:(kc + 1) * 128], tp[:, kc * 128:(kc + 1) * 128], g_sb[:, kc:kc + 1], None, ALU.mult), outs=[hb], ins=[tp, g_sb])
            fw.dma(fw.pool, self.hT_d[i], hb[:], out_t=self.hT_d, in_t=hb)
        fw.pop_scope()

    def setup(self, ccols, prep=False):
        self.ccols = ccols
        a, b = ccols["ident"]
        self.ident = self.to_bf16("ident_b", [128, 128], self.cst[:, a:b], self.cst)
        if prep:
            self.prep_consts()

    def prep_consts(self):
        fw, nc, C, cst = self.fw, self.nc, self.C, self.cst
        flag = C("flag")
        for d in range(2):
            for nm, om, l0 in ((f"A_lbB{d}", f"A_omlB{d}", f"A_l0B{d}"), (f"A_lbp{d}", f"A_omlp{d}", f"A_l0p{d}")):
                fw.op(fw.dve, lambda: nc.vector.tensor_tensor(C(nm), C(nm), C(l0), ALU.subtract), outs=[cst], ins=[cst])
                fw.op(fw.act, lambda: nc.scalar.activation(C(nm), C(nm), AF.Sigmoid), outs=[cst], ins=[cst])
                fw.op(fw.dve, lambda: nc.vector.tensor_scalar(C(nm), C(nm), flag, None, ALU.mult), outs=[cst], ins=[cst])
                fw.op(fw.dve, lambda: nc.vector.tensor_scalar(C(om), C(nm), -1.0, 1.0, ALU.mult, ALU.add), outs=[cst], ins=[cst])
        fw.op(fw.dve, lambda: nc.vector.tensor_tensor(C("C_coef0"), C("C_coef1"), C("C_coef2"), ALU.add), outs=[cst], ins=[cst])
        fw.op(fw.dve, lambda: nc.vector.tensor_scalar(C("C_coef0"), C("C_coef0"), -1.0, 1.0, ALU.mult, ALU.add), outs=[cst], ins=[cst])
        fw.op(fw.dve, lambda: nc.vector.tensor_scalar(C("C_omka"), C("C_ka"), -1.0, 1.0, ALU.mult, ALU.add), outs=[cst], ins=[cst])

    def proj_fm(self, ps, ps_ap, wb, c0, hT):
        for kc in range(KC):
            self.fw.mm(ps, ps_ap, wb, wb[:, kc, c0:c0 + 128], hT, hT[:, kc * 128:(kc + 1) * 128], start=(kc == 0), stop=(kc == KC - 1))

    def proj_tm(self, ps, ps_ap, wb, c0, ncol, hT):
        for kc in range(KC):
            self.fw.mm(ps, ps_ap, hT, hT[:, kc * 128:(kc + 1) * 128], wb, wb[:, kc, c0:c0 + ncol], start=(kc == 0), stop=(kc == KC - 1))

    def phaseB(self):
        fw, nc, NT = self.fw, self.nc, self.NT
        fw.push_scope()
        wb = self.load_w("wB_b", self.wB_d, 768)
        Pm = self.to_bf16("B_Pm_b", [128, 128], self.C("B_Pm"), self.cst)
        cst = self.cst
        Dm, rowf, rowb = self.C("B_D"), self.C("B_rowf"), self.C("B_rowb")
        colf, colb, gL = self.C("B_colf"), self.C("B_colb"), self.C("B_gL")
        gng, gnb, eps = self.C("B_gng"), self.C("B_gnb"), self.C("B_eps")
        ident = self.ident
        Gall = fw.sb("B_Gall", [128, NT, 256], BF16)
        G = fw.sb("B_G", [128, 256], F32)
        Fm = fw.sb("B_F", [128, 256], F32)
        Fb = fw.sb("B_Fb", [128, 256], BF16)
        hTs = [fw.sb(f"B_hT{i}", [128, KC * 128], BF16) for i in range(2)]
        rc = [fw.sb(f"B_rc{i}", [128, 128], F32) for i in range(2)]
        rs = [fw.sb(f"B_rs{i}", [128, 128], F32) for i in range(2)]
        zb = fw.sb("B_zb", [128, 128], BF16)
        t1 = fw.sb("B_t1", [128, 128], F32)
        t2 = fw.sb("B_t2", [128, 128], F32)
        krot = fw.sb("B_krot", [128, 128], BF16)
        qrot = fw.sb("B_qrot", [128, 128], BF16)
        qf = fw.sb("B_qf", [128, 128], BF16)
        qb = fw.sb("B_qb", [128, 128], BF16)
        ktok = fw.sb("B_ktok", [128, 128], BF16)
        Vb = fw.sb("B_V", [128, 256], BF16)
        PT = fw.sb("B_PT", [128, 128], BF16)
        sg = fw.sb("B_sg", [128, 256], F32)
        on = fw.sb("B_on", [128, 256], F32)
        junk = fw.sb("B_junk", [128, 256], BF16)
        st = fw.sb("B_st", [128, 4], F32)
        yb = [fw.sb(f"B_y{i}", [128, 256], BF16) for i in range(2)]
        it = 0

        def rotary(zps, scale, out_t, pos0):
            rcb, rsb = rc[it % 2], rs[it % 2]
            fw.op(fw.act, lambda: nc.scalar.activation(zb[:], zps[:, 0:128], AF.Copy, scale=scale), outs=[zb], ins=[zps])
            sw = self.nps()
            fw.mm(sw, sw[:, 0:128], Pm, Pm[:], zb, zb[:])
            import os
            if os.environ.get("BAR136") == "1":
                fw.barrier()
            if os.environ.get("ALT1") == "1":
                fw.op(fw.dve, lambda: nc.vector.tensor_scalar(t1[:], zps[:, 0:128], scale, None, ALU.mult), outs=[t1], ins=[zps])
                fw.op(fw.dve, lambda: nc.vector.tensor_tensor(t1[:], t1[:], rcb[:], ALU.mult), outs=[t1], ins=[t1, rcb])
            elif os.environ.get("ALT1") == "2":
                fw.op(fw.dve, lambda: nc.vector.tensor_tensor(t1[:], zps[:, 0:128], rcb[:], ALU.mult), outs=[t1], ins=[zps, rcb])
            elif os.environ.get("ALT1") == "3":
                fw.op(fw.dve, lambda: nc.vector.tensor_tensor(t1[:], zps[:, 0:128], rcb[:], ALU.mult), outs=[t1], ins=[zps, rcb, zb])
            else:
                fw.op(fw.dve, lambda: nc.vector.scalar_tensor_tensor(t1[:], zps[:, 0:128], scale, rcb[:], ALU.mult, ALU.mult), outs=[t1], ins=[zps, rcb])
            fw.op(fw.dve, lambda: nc.vector.tensor_tensor(t2[:], sw[:, 0:128], rsb[:], ALU.mult), outs=[t2], ins=[sw, rsb])
            fw.op(fw.dve, lambda: nc.vector.tensor_tensor(out_t[:], t1[:], t2[:], ALU.add), outs=[out_t], ins=[t1, t2])

        for b in range(self.NB):
            fw.op(fw.dve, lambda: nc.vector.memset(G[:], 0.0), outs=[G])
            for n in range(NT - 1, -1, -1):
                tile = b * NT + n
                hT = hTs[it % 2]
                fw.dma(fw.sp, hT[:], self.hT_d[tile], out_t=hT, in_t=self.hT_d)
                fw.dma(fw.sp, rc[it % 2][:], self.rotC[:, n * 128:(n + 1) * 128], out_t=rc[it % 2], in_t=self.rotC)
                fw.dma(fw.sp, rs[it % 2][:], self.rotS[:, n * 128:(n + 1) * 128], out_t=rs[it % 2], in_t=self.rotS)
                fw.op(fw.act, lambda: nc.scalar.copy(Gall[:, n, :], G[:]), outs=[Gall], ins=[G])
                kps = self.nps()
                self.proj_fm(kps, kps[:, 0:128], wb, 128, hT)
                vps = self.nps()
                self.proj_tm(vps, vps[:, 0:256], wb, 256, 256, hT)
                rotary(kps, SCALE_B, krot, n * 128)
                fw.op(fw.act, lambda: nc.scalar.copy(Vb[:], vps[:, 0:256]), outs=[Vb], ins=[vps])
                tp = self.psb[it % 2]
                fw.transpose(tp, tp[:, 0:128], krot, krot[:], ident, ident[:])
                fw.op(fw.act, lambda: nc.scalar.activation(ktok[:], tp[:, 0:128], AF.Copy, scale=colb), outs=[ktok], ins=[tp, cst])
                ups = self.nps()
                fw.mm(ups, ups[:, 0:256], ktok, ktok[:], Vb, Vb[:])
                fw.op(fw.dve, lambda: nc.vector.scalar_tensor_tensor(G[:], G[:], gL, ups[:, 0:256], ALU.mult, ALU.add), outs=[G], ins=[G, ups, cst])
                it += 1
            fw.op(fw.dve, lambda: nc.vector.memset(Fm[:], 0.0), outs=[Fm])
            fw.op(fw.dve, lambda: nc.vector.memset(Fb[:], 0.0), outs=[Fb])
            for n in range(NT):
                tile = b * NT + n
                hT = hTs[it % 2]
                fw.dma(fw.sp, hT[:], self.hT_d[tile], out_t=hT, in_t=self.hT_d)
                fw.dma(fw.sp, rc[it % 2][:], self.rotC[:, n * 128:(n + 1) * 128], out_t=rc[it % 2], in_t=self.rotC)
                fw.dma(fw.sp, rs[it % 2][:], self.rotS[:, n * 128:(n + 1) * 128], out_t=rs[it % 2], in_t=self.rotS)
                qps = self.nps()
                self.proj_fm(qps, qps[:, 0:128], wb, 0, hT)
                kps = self.nps()
                self.proj_fm(kps, kps[:, 0:128], wb, 128, hT)
                vps = self.nps()
                self.proj_tm(vps, vps[:, 0:512], wb, 256, 512, hT)
                rotary(qps, 1.0, qrot, n * 128)
                rotary(kps, SCALE_B, krot, n * 128)
                fw.op(fw.act, lambda: nc.scalar.copy(Vb[:], vps[:, 0:256]), outs=[Vb], ins=[vps])
                fw.op(fw.act, lambda: nc.scalar.activation(sg[:], vps[:, 256:512], AF.Silu), outs=[sg], ins=[vps])
                sps = self.nps()
                fw.mm(sps, sps[:, 0:128], krot, krot[:], qrot, qrot[:])
                fw.op(fw.dve, lambda: nc.vector.tensor_tensor(PT[:], sps[:, 0:128], Dm, ALU.mult), outs=[PT], ins=[sps, cst])
                fw.op(fw.pv, lambda: fw.pv.h.tensor_tensor(qf[:], qrot[:], rowf, ALU.mult), outs=[qf], ins=[qrot, cst])
                fw.op(fw.pv, lambda: fw.pv.h.tensor_tensor(qb[:], qrot[:], rowb, ALU.mult), outs=[qb], ins=[qrot, cst])
                ops = self.nps()
                fw.mm(ops, ops[:, 0:256], PT, PT[:], Vb, Vb[:], start=True, stop=False)
                fw.mm(ops, ops[:, 0:256], qf, qf[:], Fb, Fb[:], start=False, stop=False)
                fw.mm(ops, ops[:, 0:256], qb, qb[:], Gall, Gall[:, n, :], start=False, stop=True)
                tp = self.psb[it % 2]
                fw.transpose(tp, tp[:, 0:128], krot, krot[:], ident, ident[:])
                fw.op(fw.act, lambda: nc.scalar.activation(ktok[:], tp[:, 0:128], AF.Copy, scale=colf), outs=[ktok], ins=[tp, cst])
                ups = self.nps()
                fw.mm(ups, ups[:, 0:256], ktok, ktok[:], Vb, Vb[:])
                fw.op(fw.dve, lambda: nc.vector.scalar_tensor_tensor(Fm[:], Fm[:], gL, ups[:, 0:256], ALU.mult, ALU.add), outs=[Fm], ins=[Fm, ups, cst])
                fw.op(fw.act, lambda: nc.scalar.copy(Fb[:], Fm[:]), outs=[Fb], ins=[Fm])
                y = yb[it % 2]
                fw.op(fw.act, lambda: nc.scalar.activation(junk[:], ops[:, 0:256], AF.Copy, accum_out=st[:, 0:1]), outs=[junk, st], ins=[ops])
                fw.op(fw.dve, lambda: nc.vector.tensor_scalar(st[:, 1:2], st[:, 0:1], 1.0 / 256, None, ALU.mult), outs=[st], ins=[st])
                fw.op(fw.dve, lambda: nc.vector.tensor_scalar(on[:], ops[:, 0:256], st[:, 1:2], None, ALU.subtract), outs=[on], ins=[ops, st])
                fw.op(fw.act, lambda: nc.scalar.activation(junk[:], on[:], AF.Square, accum_out=st[:, 2:3]), outs=[junk, st], ins=[on])
                fw.op(fw.act, lambda: nc.scalar.activation(st[:, 3:4], st[:, 2:3], AF.Sqrt, bias=eps, scale=1.0 / 256), outs=[st], ins=[st, cst])
                fw.op(fw.dve, lambda: nc.vector.reciprocal(st[:, 3:4], st[:, 3:4]), outs=[st], ins=[st])
                fw.op(fw.dve, lambda: nc.vector.scalar_tensor_tensor(on[:], on[:], st[:, 3:4], gng, ALU.mult, ALU.mult), outs=[on], ins=[on, st, cst])
                fw.op(fw.dve, lambda: nc.vector.tensor_tensor(on[:], on[:], gnb, ALU.add), outs=[on], ins=[on, cst])
                fw.op(fw.dve, lambda: nc.vector.tensor_tensor(y[:], on[:], sg[:], ALU.mult), outs=[y], ins=[on, sg])
                fw.dma(fw.pool, self.y_o[tile * 128:(tile + 1) * 128, 128:384], y[:], out_t=self.y_o, in_t=y)
                it += 1
        fw.pop_scope()

    def finish(self):
        self.fw.finish([self.y_o.last_w])
        return self.nc


F_MIN = float(np.float32(1e-6))


def consts_A(c, lg, norm_g, flag):
    t = np.arange(128)
    same = (t[:, None] // 32) == (t[None, :] // 32)
    c.add("flag", np.full((128, 1), float(flag)))
    for d in range(2):
        c.add(f"A_lbB{d}", bcast(lg[1, d]))
        c.add(f"A_l0B{d}", bcast(lg[0, d]))
        c.add(f"A_omlB{d}", np.zeros((128, 128)))
        c.add(f"A_lbp{d}", lg[1, d].reshape(128, 1))
        c.add(f"A_l0p{d}", lg[0, d].reshape(128, 1))
        c.add(f"A_omlp{d}", np.zeros((128, 1)))
        if d == 0:
            tri = same & (t[:, None] <= t[None, :])
            rev = same & (t[:, None] > t[None, :])
            msk = same & (t[:, None] <= t[None, :])
        else:
            tri = same & (t[:, None] >= t[None, :])
            rev = same & (t[:, None] < t[None, :])
            msk = same & (t[:, None] >= t[None, :])
        c.add(f"A_tri{d}", tri.astype(np.float32))
        c.add(f"A_rev{d}", rev.astype(np.float32))
        c.add(f"A_msk{d}", msk.astype(np.float32))
    ind = (t[:, None] // 32 == np.arange(4)[None, :]).astype(np.float32)
    c.add("A_ind", ind)
    c.add("A_rowmask", np.repeat(ind, 128, axis=1))
    colmask = np.zeros((4, 128), np.float32)
    for k in range(4):
        colmask[k, k * 32:(k + 1) * 32] = 1
    c.add("A_colmask", bcast(colmask.reshape(-1)))
    c.add("A_ng", bcast(norm_g))
    c.add("A_eps", np.full((128, 1), 1e-6))


def phaseA(self):
    fw, nc, NT = self.fw, self.nc, self.NT
    fw.push_scope()
    wb = self.load_w("wA_b", self.wA_d, 640)
    cst = self.cst
    C = self.C
    o_bwd = fw.sb("A_obwd", [128, NT, 128], F32)
    hTs = [fw.sb(f"A_hT{i}", [128, KC * 128], BF16) for i in range(2)]
    S = [fw.sb(f"A_S{i}", [128, 128], BF16) for i in range(2)]
    sg = fw.sb("A_sg", [128, 128], F32)
    fv = fw.sb("A_fv", [128, 128], F32)
    logf = fw.sb("A_logf", [128, 128], F32)
    ktok = fw.sb("A_ktok", [128, 128], F32)
    erev = fw.sb("A_erev", [128, 128], F32)
    kt = fw.sb("A_kt", [128, 128], BF16)
    ktm = fw.sb("A_ktm", [128, 4, 128], BF16)
    Vb = fw.sb("A_V", [128, 128], BF16)
    gs = fw.sb("A_gs", [128, 128], F32)
    qs = fw.sb("A_qs", [128, 128], F32)
    e1 = fw.sb("A_e1", [128, 128], F32)
    e2 = fw.sb("A_e2", [128, 128], F32)
    sgT = fw.sb("A_sgT", [128, 128], F32)
    qbar = fw.sb("A_qbar", [128, 128], BF16)
    qbm = fw.sb("A_qbm", [128, 4, 128], BF16)
    kbar = fw.sb("A_kbar", [128, 128], BF16)
    dec = fw.sb("A_dec", [128, 4], F32)
    PT = fw.sb("A_PT", [128, 128], BF16)
    osum = fw.sb("A_osum", [128, 128], F32)
    junk = fw.sb("A_junk", [128, 128], BF16)
    st = fw.sb("A_st", [128, 2], F32)
    ys = [fw.sb(f"A_y{i}", [128, 128], BF16) for i in range(2)]
    it = 0
    si = 0
    for b in range(self.NB):
        for dr in (1, 0):
            fcol = 128 if dr == 0 else 256
            tri, rev, msk = C(f"A_tri{dr}"), C(f"A_rev{dr}"), C(f"A_msk{dr}")
            lbB, omlB, omlp = C(f"A_lbB{dr}"), C(f"A_omlB{dr}"), C(f"A_omlp{dr}")
            fw.op(fw.dve, lambda: nc.vector.memset(S[si % 2][:], 0.0), outs=[S[si % 2]])
            tiles = range(NT) if dr == 0 else range(NT - 1, -1, -1)
            for n in tiles:
                tile = b * NT + n
                hT = hTs[it % 2]
                fw.dma(fw.sp, hT[:], self.hT_d[tile], out_t=hT, in_t=self.hT_d)
                qps = self.nps()
                self.proj_fm(qps, qps[:, 0:128], wb, 0, hT)
                fps = self.nps()
                self.proj_fm(fps, fps[:, 0:128], wb, fcol, hT)
                tps = self.nps()
                self.proj_tm(tps, tps[:, 0:128], wb, fcol, 128, hT)
                ncol = 256 if dr == 0 else 128
                self.proj_tm(tps, tps[:, 128:128 + ncol], wb, 384, ncol, hT)
                fw.op(fw.act, lambda: nc.scalar.activation(sg[:], tps[:, 0:128], AF.Sigmoid), outs=[sg], ins=[tps])
                fw.op(fw.pv, lambda: fw.pv.h.tensor_tensor(fv[:], sg[:], omlB, ALU.mult), outs=[fv], ins=[sg, cst])
                fw.op(fw.pv, lambda: fw.pv.h.tensor_tensor(fv[:], fv[:], lbB, ALU.add), outs=[fv], ins=[fv, cst])
                fw.op(fw.pv, lambda: fw.pv.h.tensor_scalar(fv[:], fv[:], F_MIN, None, ALU.max), outs=[fv], ins=[fv])
                fw.op(fw.act, lambda: nc.scalar.activation(logf[:], fv[:], AF.Ln), outs=[logf], ins=[fv])
                fw.op(fw.pv, lambda: fw.pv.h.tensor_scalar(ktok[:], sg[:], -1.0, 1.0, ALU.mult, ALU.add), outs=[ktok], ins=[sg])
                fw.op(fw.pv, lambda: fw.pv.h.tensor_tensor(ktok[:], ktok[:], omlB, ALU.mult), outs=[ktok], ins=[ktok, cst])
                fw.op(fw.act, lambda: nc.scalar.copy(Vb[:], tps[:, 128:256]), outs=[Vb], ins=[tps])
                if dr == 0:
                    fw.op(fw.act, lambda: nc.scalar.activation(gs[:], tps[:, 256:384], AF.Silu), outs=[gs], ins=[tps])
                cps = self.nps()
                fw.mm(cps, cps[:, 0:128], cst, rev, logf, logf[:])
                fw.mm(cps, cps[:, 128:256], logf, logf[:], cst, tri)
                fw.mm(cps, cps[:, 256:260], logf, logf[:], cst, C("A_ind"))
                fw.op(fw.act, lambda: nc.scalar.activation(erev[:], cps[:, 0:128], AF.Exp), outs=[erev], ins=[cps])
                fw.op(fw.act, lambda: nc.scalar.activation(e1[:], cps[:, 128:256], AF.Exp), outs=[e1], ins=[cps])
                fw.op(fw.act, lambda: nc.scalar.activation(e2[:], cps[:, 128:256], AF.Exp, scale=-1.0), outs=[e2], ins=[cps])
                fw.op(fw.act, lambda: nc.scalar.activation(dec[:], cps[:, 256:260], AF.Exp), outs=[dec], ins=[cps])
                fw.op(fw.dve, lambda: nc.vector.tensor_tensor(kt[:], ktok[:], erev[:], ALU.mult), outs=[kt], ins=[ktok, erev])
                fw.op(fw.pv, lambda: fw.pv.h.tensor_tensor(ktm[:], kt[:].unsqueeze(1).broadcast_to([128, 4, 128]),
                                                               C("A_rowmask").rearrange("p (c d) -> p c d", c=4), ALU.mult), outs=[ktm], ins=[kt, cst])
                fw.op(fw.act, lambda: nc.scalar.activation(qs[:], qps[:, 0:128], AF.Silu), outs=[qs], ins=[qps])
                fw.op(fw.act, lambda: nc.scalar.activation(sgT[:], fps[:, 0:128], AF.Sigmoid, scale=-1.0), outs=[sgT], ins=[fps])
                fw.op(fw.dve, lambda: nc.vector.tensor_tensor(qbar[:], qs[:], e1[:], ALU.mult), outs=[qbar], ins=[qs, e1])
                fw.op(fw.pv, lambda: fw.pv.h.tensor_tensor(qbm[:], qbar[:].unsqueeze(1).broadcast_to([128, 4, 128]),
                                                               C("A_colmask").rearrange("p (c d) -> p c d", c=4), ALU.mult), outs=[qbm], ins=[qbar, cst])
                fw.op(fw.dve, lambda: nc.vector.scalar_tensor_tensor(kbar[:], sgT[:], omlp, e2[:], ALU.mult, ALU.mult), outs=[kbar], ins=[sgT, e2, cst])
                sps = self.nps()
                fw.mm(sps, sps[:, 0:128], kbar, kbar[:], qbar, qbar[:])
                fw.op(fw.dve, lambda: nc.vector.tensor_tensor(PT[:], sps[:, 0:128], msk, ALU.mult), outs=[PT], ins=[sps, cst])
                ops = self.nps()
                fw.mm(ops, ops[:, 0:128], PT, PT[:], Vb, Vb[:], start=True, stop=False)
                chunks = range(4) if dr == 0 else range(3, -1, -1)
                for ci, c in enumerate(chunks):
                    Sc, Sn = S[si % 2], S[(si + 1) % 2]
                    fw.mm(ops, ops[:, 0:128], qbm, qbm[:, c, :], Sc, Sc[:], start=False, stop=(ci == 3))
                    ups = self.nps()
                    fw.mm(ups, ups[:, 0:128], ktm, ktm[:, c, :], Vb, Vb[:])
                    fw.op(fw.dve, lambda: nc.vector.scalar_tensor_tensor(Sn[:], Sc[:], dec[:, c:c + 1], ups[:, 0:128], ALU.mult, ALU.add), outs=[Sn], ins=[Sc, dec, ups])
                    si += 1
                if dr == 1:
                    fw.op(fw.act, lambda: nc.scalar.copy(o_bwd[:, n, :], ops[:, 0:128]), outs=[o_bwd], ins=[ops])
                else:
                    y = ys[it % 2]
                    fw.op(fw.dve, lambda: nc.vector.tensor_tensor(osum[:], ops[:, 0:128], o_bwd[:, n, :], ALU.add), outs=[osum], ins=[ops, o_bwd])
                    fw.op(fw.act, lambda: nc.scalar.activation(junk[:], osum[:], AF.Square, accum_out=st[:, 0:1]), outs=[junk, st], ins=[osum])
                    fw.op(fw.act, lambda: nc.scalar.activation(st[:, 1:2], st[:, 0:1], AF.Sqrt, bias=C("A_eps"), scale=1.0 / 128), outs=[st], ins=[st, cst])
                    fw.op(fw.dve, lambda: nc.vector.reciprocal(st[:, 1:2], st[:, 1:2]), outs=[st], ins=[st])
                    fw.op(fw.dve, lambda: nc.vector.scalar_tensor_tensor(osum[:], osum[:], st[:, 1:2], C("A_ng"), ALU.mult, ALU.mult), outs=[osum], ins=[osum, st, cst])
                    fw.op(fw.dve, lambda: nc.vector.tensor_tensor(y[:], osum[:], gs[:], ALU.mult), outs=[y], ins=[osum, gs])
                    fw.dma(fw.pool, self.y_o[tile * 128:(tile + 1) * 128, 0:128], y[:], out_t=self.y_o, in_t=y)
                it += 1
    fw.pop_scope()


LA.phaseA = phaseA


KAPPA = -float(np.exp(-0.5))


def consts_C(c, mu, w0, wlb, a0, alb, k_k, k_a, r_k, ln_g, ln_b):
    c.add("C_coef0", np.zeros((128, 640)))
    c.add("C_coef1", bcast(mu[0]))
    c.add("C_coef2", bcast(mu[1]))
    for d in range(2):
        c.add(f"C_w0B{d}", bcast(w0[d]))
        c.add(f"C_a0p{d}", a0[d].reshape(128, 1))
        c.add(f"C_wlb{d}", np.concatenate([wlb[d], np.zeros((64, 128), np.float32)], 0))
        c.add(f"C_alb{d}", np.concatenate([alb[d], np.zeros((64, 128), np.float32)], 0))
    c.add("C_kk", k_k.reshape(128, 1))
    c.add("C_ka", k_a.reshape(128, 1))
    c.add("C_omka", np.zeros((128, 1)))
    c.add("C_rk", r_k.reshape(128, 1))
    c.add("C_e12", np.full((128, 1), 1e-12))
    c.add("C_gneps", np.full((128, 1), 64e-5))
    t = np.arange(128)
    same = (t[:, None] // 64) == (t[None, :] // 64)
    lt = t[:, None] < t[None, :]
    gt = t[:, None] > t[None, :]
    eq = t[:, None] == t[None, :]
    for d in range(2):
        early = lt if d == 0 else gt
        late = gt if d == 0 else lt
        c.add(f"C_triI{d}", KAPPA * (same & (early | eq)))
        c.add(f"C_triE{d}", KAPPA * (same & early))
        c.add(f"C_triR{d}", KAPPA * (same & late))
        mst = (same & early).astype(np.float32)
        minc = (same & (early | eq)).astype(np.float32)
        c.add(f"C_mXA{d}", np.concatenate([mst, minc], 1))
        c.add(f"C_mKA{d}", np.concatenate([-mst, minc], 1))
        c.add(f"C_mL{d}", (same & late).astype(np.float32))
    c.add("C_ind2", KAPPA * (t[:, None] // 64 == np.arange(2)[None, :]))
    blk = ((t[:, None] // 64) == (t[None, :] // 64)).astype(np.float32)
    c.add("C_blk", blk)
    c.add("C_nblk", -blk)
    c.add("C_hind", (t[:, None] // 64 == np.arange(2)[None, :]).astype(np.float32))
    cm = np.zeros((2, 128), np.float32)
    cm[0, :64] = 1
    cm[1, 64:] = 1
    c.add("C_colmask2", bcast(cm.reshape(-1)))
    c.add("C_lng", bcast(ln_g))
    c.add("C_lnb", bcast(ln_b))


def phaseC(self):
    fw, nc, NT = self.fw, self.nc, self.NT
    fw.push_scope()
    cst, C = self.cst, self.C
    ident = self.ident
    wv = [fw.sb(f"C_w{v}", [128, KC, 640], BF16) for v in range(3)]
    wg = fw.sb("C_wg", [128, KC, 128], BF16)
    for c0 in range(0, 768, 128):
        stg = self.stage[self.stage_i % 2]
        self.stage_i += 1
        fw.dma(fw.sp, stg[:], self.wC_d[:, :, c0:c0 + 128], out_t=stg, in_t=self.wC_d)
        if c0 < 640:
            for v in range(3):
                a, b_ = self.ccols[f"C_coef{v}"]
                coef = cst[:, a + c0:a + c0 + 128].unsqueeze(1).broadcast_to([128, KC, 128])
                fw.op(fw.dve, lambda: nc.vector.tensor_tensor(wv[v][:, :, c0:c0 + 128], stg[:], coef, ALU.mult), outs=[wv[v]], ins=[stg, cst])
        else:
            fw.op(fw.act, lambda: nc.scalar.copy(wg[:], stg[:]), outs=[wg], ins=[stg])
    lwb = [self.to_bf16(f"C_lwb{d}", [128, 128], C(f"C_wlb{d}"), cst) for d in range(2)]
    lab = [self.to_bf16(f"C_lab{d}", [128, 128], C(f"C_alb{d}"), cst) for d in range(2)]
    hind = self.to_bf16("C_hind_b", [128, 2], C("C_hind"), cst)
    o_bwd = fw.sb("C_obwd", [128, NT, 128], F32)
    bon_bwd = fw.sb("C_bonb", [128, NT, 2], F32)
    hTs = [fw.sb(f"C_hT{i}", [128, KC * 128], BF16) for i in range(4)]
    H = [fw.sb(f"C_H{i}", [128, 64], BF16) for i in range(2)]

    def sbt(name, shape, dt=F32):
        return fw.sb("C_" + name, shape, dt)
    rT, kT, vTb = sbt("rT", [128, 128]), sbt("kT", [128, 128]), sbt("vTb", [128, 128], BF16)
    thx, acx = sbt("thx", [64, 128], BF16), sbt("acx", [64, 128], BF16)
    aT, sq, rinv = sbt("aT", [128, 128]), sbt("sq", [128, 128]), sbt("rinv", [128, 128])
    kkT, kaT, fac, kdT = sbt("kkT", [128, 128]), sbt("kaT", [128, 128]), sbt("fac", [128, 128]), sbt("kdT", [128, 128])
    kkTb, kaTb, kdTb = sbt("kkTb", [128, 128], BF16), sbt("kaTb", [128, 128], BF16), sbt("kdTb", [128, 128], BF16)
    rkT = sbt("rkT", [128, 128], BF16)
    wtok, ld = sbt("wtok", [128, 128]), sbt("ld", [128, 128])
    eI, eE, eN = sbt("eI", [128, 128]), sbt("eE", [128, 128]), sbt("eN", [128, 128])
    eR, eEt = sbt("eR", [128, 128]), sbt("eEt", [128, 128])
    egC = sbt("egC", [128, 2])
    arT = sbt("arT", [128, 2, 128], BF16)
    bbT, kbT = sbt("bbT", [128, 128], BF16), sbt("kbT", [128, 128], BF16)
    rbF = sbt("rbF", [128, 128])
    Vtok = sbt("Vtok", [128, 128], BF16)
    Bt, Kt = sbt("Bt", [128, 128], BF16), sbt("Kt", [128, 128], BF16)
    BtP = [sbt(f"BtP{h}", [128, 128], BF16) for h in range(2)]
    KtP = [sbt(f"KtP{h}", [128, 128], BF16) for h in range(2)]
    AtP = [sbt(f"AtP{h}", [128, 128], BF16) for h in range(2)]
    WA = [sbt(f"WA{h}", [128, 128], BF16) for h in range(2)]
    U0 = [sbt(f"U0{h}", [128, 64], BF16) for h in range(2)]
    XR = [sbt(f"XR{h}", [128, 2, 128], BF16) for h in range(2)]
    KR = [sbt(f"KR{h}", [128, 2, 128], BF16) for h in range(2)]
    Lm = [sbt(f"L{h}", [128, 128], BF16) for h in range(2)]
    YZ = [[sbt(f"YZ{h}{i}", [128, 2, 128], BF16) for i in range(2)] for h in range(2)]
    YT = [[sbt(f"YT{h}{i}", [128, 128], BF16) for i in range(2)] for h in range(2)]
    RpT = sbt("RpT", [128, 128], BF16)
    RpTm = sbt("RpTm", [128, 2, 128], BF16)
    MT = [sbt(f"MT{q}", [128, 128], BF16) for q in range(2)]
    mtmp = sbt("mtmp", [128, 128])
    bon = sbt("bon", [128, 2])
    gs = sbt("gs", [128, 128])
    osum, xc, sqo = sbt("osum", [128, 128]), sbt("xc", [128, 128]), sbt("sqo", [128, 128])
    st = sbt("st", [128, 8])
    ys = [sbt(f"y{i}", [128, 128], BF16) for i in range(2)]
    for h in range(2):
        for tt in (BtP[h], KtP[h], AtP[h]):
            fw.op(fw.pv, lambda: fw.pv.h.memset(tt[:], 0.0), outs=[tt])
    identf = C("ident")
    hi = 0
    it = 0

    def evac(eng, out_t, out_ap, in_t, in_ap):
        if eng is fw.act:
            fw.op(eng, lambda: nc.scalar.copy(out_ap, in_ap), outs=[out_t], ins=[in_t])
        else:
            fw.op(eng, lambda: nc.vector.tensor_copy(out_ap, in_ap), outs=[out_t], ins=[in_t])

    for b in range(self.NB):
        for dr in (1, 0):
            tiles = list(range(NT)) if dr == 0 else list(range(NT - 1, -1, -1))
            triI, triE, triR = C(f"C_triI{dr}"), C(f"C_triE{dr}"), C(f"C_triR{dr}")
            mXA, mKA, mL = C(f"C_mXA{dr}"), C(f"C_mKA{dr}"), C(f"C_mL{dr}")
            w0B, a0p = C(f"C_w0B{dr}"), C(f"C_a0p{dr}")
            fw.op(fw.dve, lambda: nc.vector.memset(H[hi % 2][:], 0.0), outs=[H[hi % 2]])
            loaded = {}

            def ensure(n):
                if 0 <= n < NT and n not in loaded:
                    t_ = hTs[n % 4]
                    fw.dma(fw.sp, t_[:], self.hT_d[b * NT + n], out_t=t_, in_t=self.hT_d)
                    loaded[n] = t_
            for n in tiles:
                tile = b * NT + n
                ensure(n - 1), ensure(n), ensure(n + 1)
                ensure(n + 2 if dr == 0 else n - 2)
                loaded.pop(n - 3 if dr == 0 else n + 3, None)
                hc, hp, hn = loaded[n], loaded.get(n - 1), loaded.get(n + 1)

                def proj_shift(ps, M, c0):
                    seq = []
                    for kc in range(KC):
                        seq.append((ps[0:M, 0:128], wv[0][:, kc, c0:c0 + M], hc, hc[:, kc * 128:(kc + 1) * 128]))
                    for kc in range(KC):
                        seq.append((ps[0:M, 1:128], wv[1][:, kc, c0:c0 + M], hc, hc[:, kc * 128:kc * 128 + 127]))
                        if hp is not None:
                            seq.append((ps[0:M, 0:1], wv[1][:, kc, c0:c0 + M], hp, hp[:, kc * 128 + 127:kc * 128 + 128]))
                    for kc in range(KC):
                        seq.append((ps[0:M, 0:127], wv[2][:, kc, c0:c0 + M], hc, hc[:, kc * 128 + 1:(kc + 1) * 128]))
                        if hn is not None:
                            seq.append((ps[0:M, 127:128], wv[2][:, kc, c0:c0 + M], hn, hn[:, kc * 128:kc * 128 + 1]))
                    for i_, (o_ap, w_ap, ht, h_ap) in enumerate(seq):
                        wt = wv[0]
                        fw.op(fw.pe, lambda: nc.tensor.matmul(o_ap, w_ap, h_ap, start=(i_ == 0), stop=(i_ == len(seq) - 1)),
                              outs=[ps], ins=[wv[0], wv[1], wv[2], ht], skip_self=True)
                rps, kps, vps = self.nps(), self.nps(), self.nps()
                proj_shift(rps, 128, 0)
                proj_shift(kps, 128, 128)
                proj_shift(vps, 128, 256)
                cps = self.nps()
                cbase = 384 + 128 * dr
                proj_shift(cps, 64, cbase)
                cps2 = self.nps()
                proj_shift(cps2, 64, cbase + 64)
                evac(fw.act, rT, rT[:], rps, rps[:, 0:128])
                evac(fw.act, kT, kT[:], kps, kps[:, 0:128])
                evac(fw.dve, vTb, vTb[:], vps, vps[:, 0:128])
                fw.op(fw.act, lambda: nc.scalar.activation(thx[:], cps[0:64, 0:128], AF.Tanh), outs=[thx], ins=[cps])
                evac(fw.dve, acx, acx[:], cps2, cps2[0:64, 0:128])
                if dr == 0:
                    gps = self.nps()
                    self.proj_tm(gps, gps[:, 0:128], wg, 0, 128, hc)
                    fw.op(fw.act, lambda: nc.scalar.activation(gs[:], gps[:, 0:128], AF.Silu), outs=[gs], ins=[gps])
                lps = self.nps()
                fw.mm(lps, lps[:, 0:128], thx, thx[:], lwb[dr], lwb[dr][0:64, :])
                fw.mm(lps, lps[:, 128:256], lab[dr], lab[dr][0:64, :], acx, acx[:])
                fw.op(fw.dve, lambda: nc.vector.tensor_tensor(wtok[:], lps[:, 0:128], w0B, ALU.add), outs=[wtok], ins=[lps, cst])
                fw.op(fw.act, lambda: nc.scalar.activation(ld[:], wtok[:], AF.Sigmoid), outs=[ld], ins=[wtok])
                fw.op(fw.act, lambda: nc.scalar.activation(aT[:], lps[:, 128:256], AF.Sigmoid, bias=a0p), outs=[aT], ins=[lps, cst])
                fw.op(fw.act, lambda: nc.scalar.activation(sq[:], kT[:], AF.Square, scale=C("C_kk")), outs=[sq], ins=[kT, cst])
                nps_ = self.nps()
                fw.mm(nps_, nps_[:, 0:128], cst, C("C_blk"), sq, sq[:])
                fw.op(fw.act, lambda: nc.scalar.activation(rinv[:], nps_[:, 0:128], AF.Sqrt, bias=C("C_e12")), outs=[rinv], ins=[nps_, cst])
                fw.op(fw.dve, lambda: nc.vector.reciprocal(rinv[:], rinv[:]), outs=[rinv], ins=[rinv])
                fw.op(fw.dve, lambda: nc.vector.scalar_tensor_tensor(kkT[:], kT[:], C("C_kk"), rinv[:], ALU.mult, ALU.mult), outs=[kkT], ins=[kT, rinv, cst])
                fw.op(fw.pv, lambda: fw.pv.h.tensor_tensor(kaT[:], kkT[:], aT[:], ALU.mult), outs=[kaT], ins=[kkT, aT])
                fw.op(fw.pv, lambda: fw.pv.h.tensor_scalar(fac[:], aT[:], C("C_ka"), C("C_omka"), ALU.mult, ALU.add), outs=[fac], ins=[aT, cst])
                fw.op(fw.pv, lambda: fw.pv.h.tensor_tensor(kdT[:], kT[:], fac[:], ALU.mult), outs=[kdT], ins=[kT, fac])
                fw.op(fw.dve, lambda: nc.vector.scalar_tensor_tensor(rkT[:], rT[:], C("C_rk"), kdT[:], ALU.mult, ALU.mult), outs=[rkT], ins=[rT, kdT, cst])
                fw.op(fw.pv, lambda: fw.pv.h.tensor_copy(kkTb[:], kkT[:]), outs=[kkTb], ins=[kkT])
                fw.op(fw.pv, lambda: fw.pv.h.tensor_copy(kaTb[:], kaT[:]), outs=[kaTb], ins=[kaT])
                fw.op(fw.pv, lambda: fw.pv.h.tensor_copy(kdTb[:], kdT[:]), outs=[kdTb], ins=[kdT])
                bps = self.nps()
                fw.mm(bps, bps[:, 0:2], rkT, rkT[:], hind, hind[:])
                evac(fw.dve, bon, bon[:], bps, bps[:, 0:2])
                cum = self.nps()
                fw.mm(cum, cum[:, 0:128], ld, ld[:], cst, triI)
                fw.mm(cum, cum[:, 128:256], ld, ld[:], cst, triE)
                fw.mm(cum, cum[:, 256:384], cst, triR, ld, ld[:])
                fw.mm(cum, cum[:, 384:512], cst, triE, ld, ld[:])
                cum2 = self.nps()
                fw.mm(cum2, cum2[:, 0:2], ld, ld[:], cst, C("C_ind2"))
                fw.op(fw.act, lambda: nc.scalar.activation(eI[:], cum[:, 0:128], AF.Exp), outs=[eI], ins=[cum])
                fw.op(fw.act, lambda: nc.scalar.activation(eN[:], cum[:, 0:128], AF.Exp, scale=-1.0), outs=[eN], ins=[cum])
                fw.op(fw.act, lambda: nc.scalar.activation(eE[:], cum[:, 128:256], AF.Exp), outs=[eE], ins=[cum])
                fw.op(fw.act, lambda: nc.scalar.activation(eR[:], cum[:, 256:384], AF.Exp), outs=[eR], ins=[cum])
                fw.op(fw.act, lambda: nc.scalar.activation(eEt[:], cum[:, 384:512], AF.Exp), outs=[eEt], ins=[cum])
                fw.op(fw.act, lambda: nc.scalar.activation(egC[:], cum2[:, 0:2], AF.Exp), outs=[egC], ins=[cum2])
                fw.op(fw.dve, lambda: nc.vector.tensor_tensor(arT[:, 0, :], kkT[:], eE[:], ALU.mult), outs=[arT], ins=[kkT, eE])
                fw.op(fw.pv, lambda: fw.pv.h.tensor_tensor(rbF[:], rT[:], eI[:], ALU.mult), outs=[rbF], ins=[rT, eI])
                fw.op(fw.pv, lambda: fw.pv.h.tensor_copy(arT[:, 1, :], rbF[:]), outs=[arT], ins=[rbF])
                fw.op(fw.dve, lambda: nc.vector.tensor_tensor(bbT[:], kaT[:], eN[:], ALU.mult), outs=[bbT], ins=[kaT, eN])
                fw.op(fw.pv, lambda: fw.pv.h.tensor_tensor(kbT[:], kdT[:], eN[:], ALU.mult), outs=[kbT], ins=[kdT, eN])
                tp = self.psb[it % 2]
                fw.transpose(tp, tp[:, 0:128], kkTb, kkTb[:], ident, ident[:])
                fw.transpose(tp, tp[:, 128:256], kaTb, kaTb[:], ident, ident[:])
                fw.transpose(tp, tp[:, 256:384], kdTb, kdTb[:], ident, ident[:])
                fw.transpose(tp, tp[:, 384:512], vTb, vTb[:], ident, ident[:])
                for h in range(2):
                    hs = slice(64 * h, 64 * h + 64)
                    fw.op(fw.dve, lambda: nc.vector.tensor_tensor(WA[h][:, 64:128], tp[:, 64 * h:64 * h + 64], eEt[:, hs], ALU.mult), outs=[WA[h]], ins=[tp, eEt])
                    fw.op(fw.dve, lambda: nc.vector.tensor_tensor(BtP[h][:, hs], tp[:, 128 + 64 * h:128 + 64 * h + 64], eR[:, hs], ALU.mult), outs=[BtP[h]], ins=[tp, eR])
                    fw.op(fw.dve, lambda: nc.vector.tensor_tensor(KtP[h][:, hs], tp[:, 256 + 64 * h:256 + 64 * h + 64], eR[:, hs], ALU.mult), outs=[KtP[h]], ins=[tp, eR])
                fw.op(fw.dve, lambda: nc.vector.tensor_tensor(Bt[:], tp[:, 128:256], eR[:], ALU.mult), outs=[Bt], ins=[tp, eR])
                evac(fw.act, Vtok, Vtok[:], tp, tp[:, 384:512])
                for h in range(2):
                    hs = slice(64 * h, 64 * h + 64)
                    xa, ka, ll = self.nps(), self.nps(), self.nps()
                    fw.mm(xa, xa[:, 0:256], bbT, bbT[hs, :], arT, arT[hs, :, :].rearrange("p a t -> p (a t)"))
                    fw.mm(ka, ka[:, 0:256], kbT, kbT[hs, :], arT, arT[hs, :, :].rearrange("p a t -> p (a t)"))
                    fw.mm(ll, ll[:, 0:128], arT, arT[hs, 0, :], bbT, bbT[hs, :])
                    fw.op(fw.dve, lambda: nc.vector.tensor_tensor(XR[h][:].rearrange("p a t -> p (a t)"), xa[:, 0:256], mXA, ALU.mult), outs=[XR[h]], ins=[xa, cst])
                    fw.op(fw.dve, lambda: nc.vector.tensor_tensor(KR[h][:].rearrange("p a t -> p (a t)"), ka[:, 0:256], mKA, ALU.mult), outs=[KR[h]], ins=[ka, cst])
                    fw.op(fw.dve, lambda: nc.vector.tensor_tensor(Lm[h][:], ll[:, 0:128], mL, ALU.mult), outs=[Lm[h]], ins=[ll, cst])
                    yz, yt = YZ[h][0], YT[h][0]
                    fw.op(fw.dve, lambda: nc.vector.scalar_tensor_tensor(yz[:, 1, :], XR[h][:, 0, :], -1.0, identf, ALU.mult, ALU.add), outs=[yz], ins=[XR[h], cst])
                    p1, p2 = self.nps(), self.nps()
                    fw.mm(p1, p1[:, 0:128], Lm[h], Lm[h][:], XR[h], XR[h][:, 0, :])
                    fw.mm(p2, p2[:, 0:128], XR[h], XR[h][:, 0, :], Lm[h], Lm[h][:])
                    evac(fw.act, yz, yz[:, 0, :], p1, p1[:, 0:128])
                    evac(fw.act, yt, yt[:], p2, p2[:, 0:128])
                    cur = 0
                    for k in range(1, 6):
                        yz, yt = YZ[h][cur], YT[h][cur]
                        yzn, ytn = YZ[h][1 - cur], YT[h][1 - cur]
                        p1 = self.nps()
                        if k < 5:
                            fw.mm(p1, p1[:, 0:256], yt, yt[:], yz, yz[:].rearrange("p a t -> p (a t)"))
                            p2 = self.nps()
                            fw.mm(p2, p2[:, 0:128], yz, yz[:, 0, :], yt, yt[:])
                            evac(fw.act, yzn, yzn[:, 0, :], p1, p1[:, 0:128])
                            fw.op(fw.dve, lambda: nc.vector.tensor_tensor(yzn[:, 1, :], yz[:, 1, :], p1[:, 128:256], ALU.add), outs=[yzn], ins=[yz, p1])
                            evac(fw.act, ytn, ytn[:], p2, p2[:, 0:128])
                        else:
                            fw.mm(p1, p1[:, 0:128], yt, yt[:], yz, yz[:, 1, :])
                            fw.op(fw.dve, lambda: nc.vector.tensor_tensor(yzn[:, 1, :], yz[:, 1, :], p1[:, 0:128], ALU.add), outs=[yzn], ins=[yz, p1])
                        cur = 1 - cur
                    Zt = YZ[h][cur]
                    wps = self.nps()
                    fw.mm(wps, wps[:, 0:64], KR[h], KR[h][:, 0, :], Vtok, Vtok[:, hs])
                    evac(fw.act, WA[h], WA[h][:, 0:64], wps, wps[:, 0:64])
                    ups = self.nps()
                    fw.mm(ups, ups[:, 0:128], Zt, Zt[:, 1, :], WA[h], WA[h][:])
                    evac(fw.act, U0[h], U0[h][:], ups, ups[:, 0:64])
                    evac(fw.dve, AtP[h], AtP[h][:, hs], ups, ups[:, 64:128])
                rp = self.nps()
                fw.mm(rp, rp[:, 0:128], AtP[0], AtP[0][:], XR[0], XR[0][:, 1, :], start=True, stop=False)
                fw.mm(rp, rp[:, 0:128], AtP[1], AtP[1][:], XR[1], XR[1][:, 1, :], start=False, stop=True)
                fw.op(fw.dve, lambda: nc.vector.tensor_tensor(RpT[:], rbF[:], rp[:, 0:128], ALU.subtract), outs=[RpT], ins=[rbF, rp])
                fw.op(fw.pv, lambda: fw.pv.h.tensor_tensor(RpTm[:], RpT[:].unsqueeze(1).broadcast_to([128, 2, 128]),
                                                               C("C_colmask2").rearrange("p (c d) -> p c d", c=2), ALU.mult), outs=[RpTm], ins=[RpT, cst])
                for q in range(2):
                    qs_ = slice(64 * q, 64 * q + 64)
                    mp = self.nps()
                    fw.mm(mp, mp[:, 0:128], AtP[0], AtP[0][qs_, :], Bt, Bt[qs_, :], start=True, stop=False)
                    fw.mm(mp, mp[:, 0:128], AtP[1], AtP[1][qs_, :], Bt, Bt[qs_, :], start=False, stop=True)
                    fw.op(fw.dve, lambda: nc.vector.tensor_tensor(mtmp[:], mp[:, 0:128], C("C_nblk"), ALU.mult), outs=[mtmp], ins=[mp, cst])
                    fw.op(fw.dve, lambda: nc.vector.scalar_tensor_tensor(MT[q][:], identf, egC[:, q:q + 1], mtmp[:], ALU.mult, ALU.add), outs=[MT[q]], ins=[egC, mtmp, cst])
                opsh = [self.nps(), self.nps()]
                for h in range(2):
                    hs = slice(64 * h, 64 * h + 64)
                    fw.mm(opsh[h], opsh[h][:, 0:64], XR[h], XR[h][:, 1, :], U0[h], U0[h][:], start=True, stop=False)
                    fw.mm(opsh[h], opsh[h][:, 0:64], KR[h], KR[h][:, 1, :], Vtok, Vtok[:, hs], start=False, stop=False)
                chunks = (0, 1) if dr == 0 else (1, 0)
                for ci, q in enumerate(chunks):
                    qs_ = slice(64 * q, 64 * q + 64)
                    Hc, Hn = H[hi % 2], H[(hi + 1) % 2]
                    for h in range(2):
                        hs = slice(64 * h, 64 * h + 64)
                        fw.mm(opsh[h], opsh[h][:, 0:64], RpTm, RpTm[hs, q, :], Hc, Hc[hs, :], start=False, stop=(ci == 1))
                    hps = self.nps()
                    fw.mm(hps, hps[:, 0:64], MT[q], MT[q][:], Hc, Hc[:], start=True, stop=False)
                    for h in range(2):
                        hs = slice(64 * h, 64 * h + 64)
                        fw.mm(hps, hps[:, 0:64], BtP[h], BtP[h][qs_, :], U0[h], U0[h][qs_, :], start=False, stop=False)
                        fw.mm(hps, hps[:, 0:64], KtP[h], KtP[h][qs_, :], Vtok, Vtok[qs_, hs], start=False, stop=(h == 1))
                    evac(fw.act, Hn, Hn[:], hps, hps[:, 0:64])
                    hi += 1
                if dr == 1:
                    for h in range(2):
                        evac(fw.act, o_bwd, o_bwd[:, n, 64 * h:64 * h + 64], opsh[h], opsh[h][:, 0:64])
                    fw.op(fw.pv, lambda: fw.pv.h.tensor_copy(bon_bwd[:, n, :], bon[:]), outs=[bon_bwd], ins=[bon])
                else:
                    y = ys[it % 2]
                    for h in range(2):
                        fw.op(fw.dve, lambda: nc.vector.tensor_tensor(osum[:, 64 * h:64 * h + 64], opsh[h][:, 0:64], o_bwd[:, n, 64 * h:64 * h + 64], ALU.add), outs=[osum], ins=[opsh[h], o_bwd])
                    fw.op(fw.dve, lambda: nc.vector.tensor_tensor(bon[:], bon[:], bon_bwd[:, n, :], ALU.add), outs=[bon], ins=[bon, bon_bwd])
                    o3 = osum[:].rearrange("p (h v) -> p h v", h=2)
                    fw.op(fw.dve, lambda: nc.vector.tensor_reduce(st[:, 0:2], o3, AX.X, ALU.add), outs=[st], ins=[osum])
                    fw.op(fw.dve, lambda: nc.vector.tensor_scalar(st[:, 2:4], st[:, 0:2], 1.0 / 64, None, ALU.mult), outs=[st], ins=[st])
                    fw.op(fw.dve, lambda: nc.vector.tensor_tensor(xc[:].rearrange("p (h v) -> p h v", h=2), o3, st[:, 2:4].unsqueeze(2).broadcast_to([128, 2, 64]), ALU.subtract), outs=[xc], ins=[osum, st])
                    fw.op(fw.act, lambda: nc.scalar.activation(sqo[:], xc[:], AF.Square), outs=[sqo], ins=[xc])
                    fw.op(fw.dve, lambda: nc.vector.tensor_reduce(st[:, 4:6], sqo[:].rearrange("p (h v) -> p h v", h=2), AX.X, ALU.add), outs=[st], ins=[sqo])
                    fw.op(fw.act, lambda: nc.scalar.activation(st[:, 6:8], st[:, 4:6], AF.Sqrt, bias=C("C_gneps"), scale=1.0 / 64), outs=[st], ins=[st, cst])
                    fw.op(fw.dve, lambda: nc.vector.reciprocal(st[:, 6:8], st[:, 6:8]), outs=[st], ins=[st])
                    fw.op(fw.dve, lambda: nc.vector.tensor_tensor(xc[:].rearrange("p (h v) -> p h v", h=2), xc[:].rearrange("p (h v) -> p h v", h=2), st[:, 6:8].unsqueeze(2).broadcast_to([128, 2, 64]), ALU.mult), outs=[xc], ins=[xc, st])
                    fw.op(fw.pv, lambda: fw.pv.h.tensor_tensor(xc[:], xc[:], C("C_lng"), ALU.mult), outs=[xc], ins=[xc, cst])
                    fw.op(fw.pv, lambda: fw.pv.h.tensor_tensor(xc[:], xc[:], C("C_lnb"), ALU.add), outs=[xc], ins=[xc, cst])
                    fw.op(fw.dve, lambda: nc.vector.tensor_tensor(sqo[:].rearrange("p (h v) -> p h v", h=2), Vtok[:].rearrange("p (h v) -> p h v", h=2), bon[:, 0:2].unsqueeze(2).broadcast_to([128, 2, 64]), ALU.mult), outs=[sqo], ins=[Vtok, bon])
                    fw.op(fw.pv, lambda: fw.pv.h.tensor_tensor(xc[:], xc[:], sqo[:], ALU.add), outs=[xc], ins=[xc, sqo])
                    fw.op(fw.dve, lambda: nc.vector.tensor_tensor(y[:], xc[:], gs[:], ALU.mult), outs=[y], ins=[xc, gs])
                    fw.dma(fw.pool, self.y_o[tile * 128:(tile + 1) * 128, 384:512], y[:], out_t=self.y_o, in_t=y)
                it += 1
    fw.pop_scope()


LA.phaseC = phaseC


GT = 256


class LB:
    def __init__(self, NTOK, final):
        nc = self.nc = bass.Bass("TRN2", target_bir_lowering=False)
        fw = self.fw = FW(nc)
        self.NTOK = NTOK
        x = fw.dram("x", [NTOK, D], F32, kind="ExternalInput")
        g_in = fw.dram("g_in", [128, KC], F32, kind="ExternalInput")
        fg_d = fw.dram("fg", [128, D], F32, kind="ExternalInput")
        ident_d = fw.dram("ident", [128, 128], F32, kind="ExternalInput")
        yT_d = fw.dram("yT", [128, 32, NTOK], BF16, kind="ExternalInput")
        wm_d = fw.dram("wm", [128, KC, 3072], F32, kind="ExternalInput")
        wbr_d = fw.dram("wbr", [128, 32, D], F32, kind="ExternalInput")
        wo_d = fw.dram("wo", [128, KC, D], F32, kind="ExternalInput")
        out_d = fw.dram("out", [NTOK, D], F32, kind="ExternalOutput")
        ps = [fw.ps(f"ps{i}", [128, 512], F32) for i in range(6)]
        psb = [fw.ps(f"psb{i}", [128, 1024], BF16) for i in range(2)]
        self.pi = 0

        def nps():
            p = ps[self.pi % 6]
            self.pi += 1
            return p
        identf = fw.sb("identf", [128, 128], F32)
        fw.dma(fw.sp, identf[:], ident_d[:], out_t=identf, in_t=ident_d)
        ident = fw.sb("ident_b", [128, 128], BF16)
        fw.op(fw.dve, lambda: nc.vector.tensor_copy(ident[:], identf[:]), outs=[ident], ins=[identf])
        g_sb = fw.sb("g_sb", [128, KC], F32)
        fw.dma(fw.sp, g_sb[:], g_in[:], out_t=g_sb, in_t=g_in)
        fg = fw.sb("fg_sb", [128, D], F32)
        fw.dma(fw.sp, fg[:], fg_d[:], out_t=fg, in_t=fg_d)
        epsc = fw.sb("epsc", [128, 1], F32)
        fw.op(fw.dve, lambda: nc.vector.memset(epsc[:], EPS), outs=[epsc])
        stage = [fw.sb(f"stg{i}", [128, 1024], F32) for i in range(2)]
        si = [0]

        def load_w(name, wd, nk, ncols):
            wb = fw.sb(name, [128, nk, ncols], BF16)
            per = 1024 // ncols if ncols < 1024 else 1
            for k in range(nk):
                for c0 in range(0, ncols, 1024):
                    st = stage[si[0] % 2]
                    si[0] += 1
                    w_ = min(1024, ncols - c0)
                    fw.dma(fw.sp, st[:, 0:w_], wd[:, k, c0:c0 + w_], out_t=st, in_t=wd)
                    if si[0] % 2 == 0:
                        fw.op(fw.act, lambda: nc.scalar.copy(wb[:, k, c0:c0 + w_], st[:, 0:w_]), outs=[wb], ins=[st])
                    else:
                        fw.op(fw.dve, lambda: nc.vector.tensor_copy(wb[:, k, c0:c0 + w_], st[:, 0:w_]), outs=[wb], ins=[st])
            return wb
        wm = load_w("wm_b", wm_d, KC, 3072)
        wbr = load_w("wbr_b", wbr_d, 32, D)
        wo = load_w("wo_b", wo_d, KC, D)
        xt = [fw.sb(f"xt{i}", [128, D], F32) for i in range(GT // 128)]
        sq = fw.sb("sq", [128, D], BF16)
        xn = fw.sb("xn", [128, D], BF16)
        ssq = fw.sb("ssq", [128, 1], F32)
        hT = fw.sb("hT", [128, KC, GT], BF16)
        yT = [fw.sb(f"yTs{i}", [128, 32, GT], BF16) for i in range(2)]
        gate = fw.sb("gate", [128, GT], F32)
        tmp = fw.sb("tmp", [128, GT], F32)
        mixf = fw.sb("mixf", [128, GT], F32)
        mixT = fw.sb("mixT", [128, KC, GT], BF16)
        xo = [fw.sb(f"xo{i}", [128, D], F32) for i in range(2)]
        st = fw.sb("st", [128, 2], F32)
        junk = fw.sb("junk", [128, D], BF16)
        br_fc = [(0, 8), (8, 24), (24, 32)]
        oi = 0
        for gi in range(NTOK // GT):
            t0 = gi * GT
            yb = yT[gi % 2]
            fw.dma(fw.sp, yb[:], yT_d[:, :, t0:t0 + GT], out_t=yb, in_t=yT_d)
            for j in range(GT // 128):
                xb = xt[j]
                fw.dma(fw.sp, xb[:], x[t0 + j * 128:t0 + (j + 1) * 128, :], out_t=xb, in_t=x)
                fw.op(fw.act, lambda: nc.scalar.activation(sq[:], xb[:], AF.Square, accum_out=ssq[:]), outs=[sq, ssq], ins=[xb])
                fw.op(fw.act, lambda: nc.scalar.activation(ssq[:], ssq[:], AF.Sqrt, bias=epsc[:], scale=1.0 / D), outs=[ssq], ins=[ssq, epsc])
                fw.op(fw.dve, lambda: nc.vector.reciprocal(ssq[:], ssq[:]), outs=[ssq], ins=[ssq])
                fw.op(fw.dve, lambda: nc.vector.tensor_scalar(xn[:], xb[:], ssq[:, 0:1], None, ALU.mult), outs=[xn], ins=[xb, ssq])
                tp = psb[j % 2]
                for kc in range(KC):
                    fw.transpose(tp, tp[:, kc * 128:(kc + 1) * 128], xn, xn[:, kc * 128:(kc + 1) * 128], ident, ident[:])
                for kc in range(KC):
                    if kc % 2 == 0:
                        fw.op(fw.act, lambda: nc.scalar.activation(hT[:, kc, j * 128:(j + 1) * 128], tp[:, kc * 128:(kc + 1) * 128], AF.Copy, scale=g_sb[:, kc:kc + 1]), outs=[hT], ins=[tp, g_sb])
                    else:
                        fw.op(fw.dve, lambda: nc.vector.tensor_scalar(hT[:, kc, j * 128:(j + 1) * 128], tp[:, kc * 128:(kc + 1) * 128], g_sb[:, kc:kc + 1], None, ALU.mult), outs=[hT], ins=[tp, g_sb])
            for nt in range(KC):
                for br in range(3):
                    gps = nps()
                    for kc in range(KC):
                        fw.mm(gps, gps[:, 0:GT], wm, wm[:, kc, br * 1024 + nt * 128: br * 1024 + (nt + 1) * 128], hT, hT[:, kc, :], start=(kc == 0), stop=(kc == KC - 1))
                    fw.op(fw.act, lambda: nc.scalar.activation(gate[:], gps[:, 0:GT], AF.Sigmoid), outs=[gate], ins=[gps])
                    bps = nps()
                    f0, f1 = br_fc[br]
                    for fc in range(f0, f1):
                        fw.mm(bps, bps[:, 0:GT], wbr, wbr[:, fc, nt * 128:(nt + 1) * 128], yb, yb[:, fc, :], start=(fc == f0), stop=(fc == f1 - 1))
                    if br == 0:
                        fw.op(fw.dve, lambda: nc.vector.tensor_tensor(mixf[:], gate[:], bps[:, 0:GT], ALU.mult), outs=[mixf], ins=[gate, bps])
                    else:
                        fw.op(fw.dve, lambda: nc.vector.tensor_tensor(tmp[:], gate[:], bps[:, 0:GT], ALU.mult), outs=[tmp], ins=[gate, bps])
                        if br == 1:
                            fw.op(fw.pv, lambda: fw.pv.h.tensor_tensor(mixf[:], mixf[:], tmp[:], ALU.add), outs=[mixf], ins=[mixf, tmp])
                        else:
                            fw.op(fw.pv, lambda: fw.pv.h.tensor_tensor(mixT[:, nt, :], mixf[:], tmp[:], ALU.add), outs=[mixT], ins=[mixf, tmp])
            for j in range(GT // 128):
                xb = xt[j]
                o = xo[oi % 2]
                oi += 1
                for half in range(2):
                    ops = nps()
                    for kc in range(KC):
                        fw.mm(ops, ops[:, 0:512], mixT, mixT[:, kc, j * 128:(j + 1) * 128], wo, wo[:, kc, half * 512:(half + 1) * 512], start=(kc == 0), stop=(kc == KC - 1))
                    fw.op(fw.dve, lambda: nc.vector.tensor_tensor(o[:, half * 512:(half + 1) * 512], ops[:, 0:512], xb[:, half * 512:(half + 1) * 512], ALU.add), outs=[o], ins=[ops, xb])
                if final:
                    fw.op(fw.act, lambda: nc.scalar.activation(junk[:], o[:], AF.Square, accum_out=st[:, 0:1]), outs=[junk, st], ins=[o])
                    fw.op(fw.act, lambda: nc.scalar.activation(st[:, 1:2], st[:, 0:1], AF.Sqrt, bias=epsc[:], scale=1.0 / D), outs=[st], ins=[st, epsc])
                    fw.op(fw.dve, lambda: nc.vector.reciprocal(st[:, 1:2], st[:, 1:2]), outs=[st], ins=[st])
                    fw.op(fw.dve, lambda: nc.vector.scalar_tensor_tensor(o[:], o[:], st[:, 1:2], fg[:], ALU.mult, ALU.mult), outs=[o], ins=[o, st, fg])
                fw.dma(fw.pool, out_d[t0 + j * 128:t0 + (j + 1) * 128, :], o[:], out_t=out_d, in_t=o)
        fw.finish([out_d.last_w])


import ml_dtypes

NCORES = 8
T_SEQ = 8192
N_BATCH = 2
IN_SIZES = (1024, 1024, 1024, 1024, 1024, 1024, 1024, 2048, 2048, 3328, 1024, 3072)
OFFS = [0] + [int(v) for v in np.cumsum(IN_SIZES)[:-1]]


def _wl(w):
    return np.ascontiguousarray(w.reshape(-1, 128, w.shape[1]).transpose(1, 0, 2))


def _core_inputs_A(c, l, P):
    w_in = P["w_in"][l]
    hs = slice(c * 128, (c + 1) * 128)

    def cols(i, a, b):
        return w_in[:, OFFS[i] + a: OFFS[i] + b]
    wA = np.concatenate([cols(i, c * 128, c * 128 + 128) for i in range(5)], axis=1)
    wB = np.concatenate([cols(5, c * 128, c * 128 + 128), cols(6, c * 128, c * 128 + 128),
                         cols(7, c * 256, c * 256 + 256), cols(8, c * 256, c * 256 + 256)], axis=1)
    r128 = np.arange(c * 128, c * 128 + 128)
    ccols = np.concatenate([r128, 1024 + r128, 2048 + r128, 3072 + np.arange(0, 64), 3072 + 128 + np.arange(0, 64),
                            3072 + 64 + np.arange(0, 64), 3072 + 128 + 64 + np.arange(0, 64)])
    wC = np.concatenate([w_in[:, OFFS[9] + ccols], cols(10, c * 128, c * 128 + 128)], axis=1)
    cc = consts_common()
    consts_A(cc, P["hgrn_lb_logits"][:, :, hs], P["hgrn_norm_g"][l][c], l)
    consts_B(cc, c, P["ret_norm_g"][l][c], P["ret_norm_b"][l][c])
    consts_C(cc, P["rwkv_mu"][l][:, ccols], P["rwkv_w0"][l][:, hs], P["rwkv_w_lora_b"][l][:, :, hs], P["rwkv_a0"][l][:, hs],
             P["rwkv_a_lora_b"][l][:, :, hs], P["rwkv_k_k"][l][hs], P["rwkv_k_a"][l][hs], P["rwkv_r_k"][l][hs],
             P["rwkv_ln_g"][l].reshape(-1)[hs], P["rwkv_ln_b"][l].reshape(-1)[hs])
    return cc, {"wA": _wl(wA), "wB": _wl(wB), "wC": _wl(wC)}


_PROGS = {}


def _get_LA(ncst, ccols):
    if "A" not in _PROGS:
        la = LA(T_SEQ, N_BATCH, ncst)
        la.setup(ccols, prep=True)
        la.phase0()
        la.phaseA()
        la.phaseB()
        la.phaseC()
        la.finish()
        _PROGS["A"] = la
    return _PROGS["A"]


def _get_LB(final):
    k = ("B", final)
    if k not in _PROGS:
        _PROGS[k] = LB(T_SEQ * N_BATCH // NCORES, final)
    return _PROGS[k]


def kernel(**inputs):
    P = {k: np.asarray(v) for k, v in inputs.items()}
    x = np.ascontiguousarray(P["x"].reshape(-1, 1024).astype(np.float32, copy=False))
    NTOK = x.shape[0]
    rc, rs = rotary_tables(T_SEQ)
    ident = np.eye(128, dtype=np.float32)
    fg = np.ascontiguousarray(np.broadcast_to(P["final_norm_g"][None, :].astype(np.float32), (128, 1024)))
    depth = P["w_in"].shape[0]
    for l in range(depth):
        g_in = np.ascontiguousarray(P["norm_g"][l].reshape(8, 128).T)
        in_maps = []
        ccols = None
        for c in range(NCORES):
            cc, wd = _core_inputs_A(c, l, P)
            ccols = cc.cols
            m = {"x": x, "g_in": g_in, "cst": cc.array(), "rotC": rc, "rotS": rs}
            m.update(wd)
            in_maps.append(m)
        la = _get_LA(cc.n, ccols)
        res = run_bass_kernel_spmd(la.nc, in_maps, core_ids=list(range(NCORES)))
        ys = [np.asarray(r["y_o"]) for r in res.results]
        y_all = np.concatenate([y[:, 0:128] for y in ys] + [y[:, 128:384] for y in ys] + [y[:, 384:512] for y in ys], axis=1)
        w_in = P["w_in"][l]
        wm = _wl(w_in[:, OFFS[11]:])
        wbr = _wl(np.concatenate([P["w_branch_a"][l], P["w_branch_b"][l], P["w_branch_c"][l]], 0))
        wo = _wl(P["w_out"][l])
        per = NTOK // NCORES
        in_maps = []
        for c in range(NCORES):
            ysh = y_all[c * per:(c + 1) * per]
            yT = np.ascontiguousarray(ysh.T.reshape(32, 128, per).transpose(1, 0, 2))
            in_maps.append({"x": np.ascontiguousarray(x[c * per:(c + 1) * per]), "g_in": g_in, "fg": fg, "ident": ident,
                            "yT": yT, "wm": wm, "wbr": wbr, "wo": wo})
        lb = _get_LB(l == depth - 1)
        res = run_bass_kernel_spmd(lb.nc, in_maps, core_ids=list(range(NCORES)))
        x = np.ascontiguousarray(np.concatenate([np.asarray(r["out"]) for r in res.results], axis=0))
    return x.reshape(P["x"].shape).astype(np.float32, copy=False)
```
